# Optimizing a Trainium2 kernel written in Bass

```python
import math
import jax, jax.numpy as jnp
from jax import lax
import numpy as np

D_MODEL = 1024
BATCH = 4
SEQ = 8192
DEPTH = 4

CHUNK = 64
MIX_WIDTH = D_MODEL
CONV_WIDTH = MIX_WIDTH // 2
CONV_GROUPS = 8
CONV_K = 31
RET_HEADS = 4
RET_DIM = (MIX_WIDTH - CONV_WIDTH) // RET_HEADS
RET_WIDTH = RET_HEADS * RET_DIM
IN_WIDTH = 2 * CONV_WIDTH + 4 * RET_WIDTH
D_FF = int(math.ceil(8 * D_MODEL / 3 / 256) * 256)
ROPE_BASE = 10000.0
EPS = 1e-6

kernel_name = 'hybrid_conformer_retention_encoder'


def rms_norm(x, g):
    xf = x.astype(jnp.float32)
    y = xf * lax.rsqrt(jnp.mean(xf * xf, axis=-1, keepdims=True) + EPS)
    return y.astype(x.dtype) * g


def layer_norm(x, g, b):
    xf = x.astype(jnp.float32)
    mu = jnp.mean(xf, axis=-1, keepdims=True)
    var = jnp.mean(jnp.square(xf - mu), axis=-1, keepdims=True)
    return ((xf - mu) * lax.rsqrt(var + EPS)).astype(x.dtype) * g + b


def causal_depthwise_conv(u, w, b):
    C = u.shape[-1]
    up = jnp.pad(u, ((0, 0), (CONV_K - 1, 0), (0, 0)))
    y = lax.conv_general_dilated(up, w[:, None, :].astype(u.dtype), window_strides=(1,),
                                 padding='VALID', dimension_numbers=('NWC', 'WIO', 'NWC'),
                                 feature_group_count=C)
    return y + b


def rotary(t, pos):
    half = t.shape[-1] // 2
    freqs = ROPE_BASE ** (-jnp.arange(half, dtype=jnp.float32) / half)
    ang = pos[:, None] * freqs[None, :]
    cos = jnp.cos(ang)[None, :, None, :]
    sin = jnp.sin(ang)[None, :, None, :]
    t1, t2 = t[..., :half], t[..., half:]
    return jnp.concatenate([t1 * cos - t2 * sin, t1 * sin + t2 * cos], axis=-1)


def chunk_retention(q, k, v):
    Bsz, S, H, Dk = q.shape
    Dv = v.shape[-1]
    NC = S // CHUNK

    def blk(t):
        return t.reshape(Bsz, NC, CHUNK, H, t.shape[-1]).transpose(0, 3, 1, 2, 4)

    q, k, v = blk(q), blk(k), blk(v)
    log_g = jnp.log(1.0 - 2.0 ** (-5.0 - jnp.arange(H, dtype=jnp.float32)))
    idx = jnp.arange(CHUNK, dtype=jnp.float32)
    intra_decay = jnp.exp(log_g[:, None, None] * jnp.abs(idx[:, None] - idx[None, :]))
    scores = jnp.einsum('bhnid,bhnjd->bhnij', q, k) * intra_decay[None, :, None]
    intra = jnp.einsum('bhnij,bhnje->bhnie', scores, v)

    k_dec = k * jnp.exp(log_g[:, None] * (CHUNK - 1 - idx))[None, :, None, :, None]
    kv = jnp.einsum('bhnjd,bhnje->nbhde', k_dec, v)
    chunk_decay = jnp.exp(log_g * CHUNK)[None, :, None, None]

    def step(state, kv_n):
        return chunk_decay * state + kv_n, state

    _, prev = lax.scan(step, jnp.zeros((Bsz, H, Dk, Dv), jnp.float32), kv)
    q_dec = q * jnp.exp(log_g[:, None] * (idx + 1.0))[None, :, None, :, None]
    cross = jnp.einsum('bhnid,nbhde->bhnie', q_dec, prev)
    return (intra + cross).transpose(0, 2, 3, 1, 4).reshape(Bsz, S, H, Dv)


def setup_inputs(seed: int = 0) -> dict:
    key = jax.random.key(seed)
    ks = jax.random.split(key, 16)
    f32 = jnp.float32
    n = lambda k, shape, s: jax.random.normal(k, shape, f32) * s
    return {
        'x': n(ks[0], (BATCH, SEQ, D_MODEL), 1.0),
        'norm1_g': 1.0 + n(ks[1], (DEPTH, D_MODEL), 0.02),
        'w_in': n(ks[2], (DEPTH, D_MODEL, IN_WIDTH), D_MODEL ** -0.5),
        'conv_w': n(ks[3], (DEPTH, CONV_K, CONV_WIDTH), CONV_K ** -0.5),
        'conv_b': n(ks[4], (DEPTH, CONV_WIDTH), 0.01),
        'conv_ln_g': 1.0 + n(ks[5], (DEPTH, CONV_WIDTH), 0.02),
        'conv_ln_b': n(ks[6], (DEPTH, CONV_WIDTH), 0.01),
        'ret_gn_g': 1.0 + n(ks[7], (DEPTH, RET_WIDTH), 0.02),
        'w_out': n(ks[8], (DEPTH, MIX_WIDTH, D_MODEL), (MIX_WIDTH * 2 * DEPTH) ** -0.5),
        'norm2_g': 1.0 + n(ks[9], (DEPTH, D_MODEL), 0.02),
        'w_gate': n(ks[10], (DEPTH, D_MODEL, D_FF), D_MODEL ** -0.5),
        'w_up': n(ks[11], (DEPTH, D_MODEL, D_FF), D_MODEL ** -0.5),
        'w_down': n(ks[12], (DEPTH, D_FF, D_MODEL), (D_FF * 2 * DEPTH) ** -0.5),
        'final_g': 1.0 + n(ks[13], (D_MODEL,), 0.02),
    }


def reference(x, norm1_g, w_in, conv_w, conv_b, conv_ln_g, conv_ln_b, ret_gn_g, w_out,
              norm2_g, w_gate, w_up, w_down, final_g):
    Bsz, S, _ = x.shape
    pos = jnp.arange(S, dtype=jnp.float32)
    cw, rw = CONV_WIDTH, RET_WIDTH
    for l in range(DEPTH):
        h = rms_norm(x, norm1_g[l])
        proj = h @ w_in[l]
        a = proj[..., :cw]
        b = proj[..., cw:2 * cw]
        q = proj[..., 2 * cw:2 * cw + rw]
        k = proj[..., 2 * cw + rw:2 * cw + 2 * rw]
        v = proj[..., 2 * cw + 2 * rw:2 * cw + 3 * rw]
        g = proj[..., 2 * cw + 3 * rw:]

        u = a * jax.nn.sigmoid(b)
        u = causal_depthwise_conv(u, conv_w[l], conv_b[l])
        u = jax.nn.silu(layer_norm(u, conv_ln_g[l], conv_ln_b[l]))

        qh = rotary(q.reshape(Bsz, S, RET_HEADS, RET_DIM).astype(jnp.float32), pos)
        kh = rotary(k.reshape(Bsz, S, RET_HEADS, RET_DIM).astype(jnp.float32), pos) * (RET_DIM ** -0.5)
        vh = v.reshape(Bsz, S, RET_HEADS, RET_DIM).astype(jnp.float32)
        r = chunk_retention(qh, kh, vh)
        mu = jnp.mean(r, axis=-1, keepdims=True)
        var = jnp.mean(jnp.square(r - mu), axis=-1, keepdims=True)
        r = ((r - mu) * lax.rsqrt(var + EPS)).reshape(Bsz, S, rw).astype(x.dtype)
        r = r * ret_gn_g[l] * jax.nn.silu(g)

        mixed = jnp.concatenate([u, r], axis=-1)
        x = x + mixed @ w_out[l]

        h2 = rms_norm(x, norm2_g[l])
        x = x + (jax.nn.silu(h2 @ w_gate[l]) * (h2 @ w_up[l])) @ w_down[l]
    return rms_norm(x, final_g)
```

```python
import math
from contextlib import ExitStack

import numpy as np
import concourse.bass as bass
import concourse.mybir as mybir
from concourse.bass_utils import run_bass_kernel_spmd

F32 = mybir.dt.float32
BF16 = mybir.dt.bfloat16
AF = mybir.ActivationFunctionType
ALU = mybir.AluOpType
AX = mybir.AxisListType

D = 1024
DEPTH = 4
CW = 512
RW = 512
NH = 4
DH = 128
CK = 31
HALO = CK - 1
DFF = 2816
NF = DFF // 128
INW = 3072
EPS = 1e-6
GAM = [1.0 - 2.0 ** (-5.0 - h) for h in range(NH)]
BLK = 512
TPB = BLK // 128

ENGS = ("pe", "act", "dve", "pool", "sp")
TICK_LIMIT = 30000
NDMA_SEMS = {'sp': 16, 'pool': 60, 'act': 4, 'pe': 4, 'dve': 4}


class Op:
    __slots__ = ("eng", "fn", "deps", "sig", "sem", "val", "is_dma", "waits", "ndep")

    def __init__(self, eng, fn, is_dma):
        self.eng = eng
        self.fn = fn
        self.deps = []
        self.sig = False
        self.sem = None
        self.val = 0
        self.is_dma = is_dma
        self.waits = []
        self.ndep = 0


class Sched:
    def __init__(self, nc):
        self.nc = nc
        self.ops = {e: [] for e in ENGS}
        self.lastw = {}
        self.readers = {}

    def add(self, eng, fn, reads=(), writes=(), dma=False):
        op = Op(eng, fn, dma)
        deps = {}
        for k in reads:
            w = self.lastw.get(k)
            if w is not None:
                deps[id(w)] = w
        for k in writes:
            w = self.lastw.get(k)
            if w is not None and (w.eng != eng or dma or w.is_dma):
                deps[id(w)] = w
            for r in self.readers.get(k, ()):
                if r.eng != eng or dma or r.is_dma:
                    deps[id(r)] = r
        for k in reads:
            self.readers.setdefault(k, []).append(op)
        for k in writes:
            self.lastw[k] = op
            self.readers[k] = []
        op.deps = list(deps.values())
        for d in op.deps:
            d.ndep += 1
        self.ops[eng].append(op)
        return op

    def emit(self, stack):
        nc = self.nc

        def newsem(name):
            return stack.enter_context(nc.semaphore(name))

        for e in ENGS:
            cnt = 0
            epoch = 0
            cur = None
            for op in self.ops[e]:
                if op.is_dma:
                    continue
                if op.ndep > 0:
                    if cur is None or cnt >= TICK_LIMIT:
                        cur = newsem(f"s_{e}_{epoch}")
                        epoch += 1
                        cnt = 0
                    cnt += 1
                    op.sig = True
                    op.sem = cur
                    op.val = cnt
        for e in ENGS:
            dmas = [op for op in self.ops[e] if op.is_dma]
            if not dmas:
                continue
            pool = [newsem(f"d_{e}_{i}") for i in range(min(NDMA_SEMS[e], len(dmas)))]
            counts = [0] * len(pool)
            prev = [None] * len(pool)
            for i, op in enumerate(dmas):
                j = i % len(pool)
                counts[j] += 1
                op.sig = True
                op.sem = pool[j]
                op.val = 16 * counts[j]
                if prev[j] is not None:
                    op.deps.append(prev[j])
                prev[j] = op
        for e in ENGS:
            best = {}
            for op in self.ops[e]:
                need = {}
                for d in op.deps:
                    k = id(d.sem)
                    if d.val > best.get(k, 0) and d.val > need.get(k, (None, 0))[1]:
                        need[k] = (d.sem, d.val)
                for k, (s, v) in need.items():
                    best[k] = v
                    op.waits.append((s, v))
        with nc.Block() as block:
            def mk(e):
                def body(engobj):
                    for op in self.ops[e]:
                        for (s, v) in op.waits:
                            engobj.wait_ge(s, v)
                        ins = op.fn(engobj)
                        if op.sig:
                            ins.then_inc(op.sem, 16 if op.is_dma else 1)
                return body
            if self.ops["pe"]:
                block.tensor(mk("pe"))
            if self.ops["act"]:
                block.scalar(mk("act"))
            if self.ops["dve"]:
                block.vector(mk("dve"))
            if self.ops["pool"]:
                block.gpsimd(mk("pool"))
            if self.ops["sp"]:
                block.sync(mk("sp"))


class Ctx:
    def __init__(self, nc, st):
        self.nc = nc
        self.st = st
        self.s = Sched(nc)
        self.bank_i = 0
        self.banks = []
        self.tbank_i = 0
        self.tbanks = []

    def sb(self, name, shape, dt):
        return self.st.enter_context(self.nc.sbuf_tensor(name, shape, dt))

    def din(self, name, shape, dt=F32):
        return self.nc.dram_tensor(name, shape, dt, kind="ExternalInput").ap()

    def dout(self, name, shape, dt=F32):
        return self.nc.dram_tensor(name, shape, dt, kind="ExternalOutput").ap()

    def dint(self, name, shape, dt):
        return self.nc.dram_tensor(name, shape, dt, kind="Internal").ap()

    def mkbanks(self, nf32, nbf):
        for i in range(nf32):
            t = self.st.enter_context(self.nc.psum_tensor(f"ps{i}", [128, 512], F32))
            self.banks.append((t, f"ps{i}"))
        for i in range(nbf):
            t = self.st.enter_context(self.nc.psum_tensor(f"pt{i}", [128, 1024], BF16))
            self.tbanks.append((t, f"pt{i}"))

    def bank(self):
        b = self.banks[self.bank_i % len(self.banks)]
        self.bank_i += 1
        return b

    def tbank(self):
        b = self.tbanks[self.tbank_i % len(self.tbanks)]
        self.tbank_i += 1
        return b

    def dma(self, out, in_, reads, writes, eng="sp"):
        self.s.add(eng, lambda e: e.dma_start(out=out, in_=in_), reads=reads, writes=writes, dma=True)

    def mm(self, out, lhsT, rhs, start, stop, reads, writes):
        self.s.add("pe", lambda e: e.matmul(out, lhsT=lhsT, rhs=rhs, start=start, stop=stop),
                   reads=reads, writes=writes)

    def tr(self, out, in_, ident, reads, writes):
        self.s.add("pe", lambda e: e.transpose(out=out, in_=in_, identity=ident), reads=reads, writes=writes)

    def act(self, out, in_, func, reads, writes, **kw):
        self.s.add("act", lambda e: e.activation(out=out, in_=in_, func=func, **kw), reads=reads, writes=writes)

    def tt(self, out, in0, in1, op, reads, writes, eng="dve"):
        self.s.add(eng, lambda e: e.tensor_tensor(out=out, in0=in0, in1=in1, op=op), reads=reads, writes=writes)

    def ts(self, out, in0, s1, s2, op0, op1, reads, writes):
        if s2 is None:
            self.s.add("dve", lambda e: e.tensor_scalar(out=out, in0=in0, scalar1=s1, scalar2=None, op0=op0),
                       reads=reads, writes=writes)
        else:
            self.s.add("dve", lambda e: e.tensor_scalar(out=out, in0=in0, scalar1=s1, scalar2=s2, op0=op0, op1=op1),
                       reads=reads, writes=writes)

    def stt(self, out, in0, scalar, in1, op0, op1, reads, writes):
        self.s.add("dve", lambda e: e.scalar_tensor_tensor(out=out, in0=in0, scalar=scalar, in1=in1, op0=op0, op1=op1),
                   reads=reads, writes=writes)


def build(kind, ntok, final=False):
    nblk = ntok // BLK
    ntile = ntok // 128
    nc = bass.Bass("TRN2", target_bir_lowering=False)
    st = ExitStack()
    c = Ctx(nc, st)
    s = c.s
    with st:
        x_d = c.din("x", [ntok, D]).rearrange("(t p) f -> p t f", p=128)
        g1_d = c.din("g1", [1, D])
        win_d = c.din("w_in", [D, INW])
        rot_d = c.din("rot", [128, ntile, 2, 64])
        win_s = c.dint("win_s", [6, 128, 8, 512], BF16)
        if kind == "A":
            d1_d = c.din("d1tab", [128, ntile, NH])
            send_d = c.dout("s_end", [128, NH, DH])
            utail_d = c.dout("u_tail", [128, 4, HALO])
        else:
            sin_d = c.din("s_in", [128, NH, DH])
            uhalo_d = c.din("u_halo", [128, 4, HALO])
            g2_d = c.din("g2", [1, D])
            cwT_d = c.din("conv_wT", [128, 4, CK])
            cb_d = c.din("conv_b", [128, 4])
            lng_d = c.din("ln_g", [128, 4])
            lnb_d = c.din("ln_b", [128, 4])
            gng_d = c.din("gn_g", [1, RW])
            wout_d = c.din("w_out", [D, D])
            wg_d = c.din("w_gate", [D, DFF])
            wu_d = c.din("w_up", [D, DFF])
            wd_d = c.din("w_down", [DFF, D])
            ctab_d = c.din("ctab", [128, 512])
            mask_d = c.din("mask", [128, 512])
            dtab_d = c.din("dtab", [128, NH])
            if final:
                gf_d = c.din("gf", [1, D])
            xo_d = c.dout("x_out", [ntok, D]).rearrange("(t p) f -> p t f", p=128)
            wout_s = c.dint("wout_s", [128, 8, D], BF16)
            wg_s = c.dint("wg_s", [11, 128, 8, 256], BF16)
            wu_s = c.dint("wu_s", [11, 128, 8, 256], BF16)
            wd_s = c.dint("wd_s", [128, NF, D], BF16)

        c.mkbanks(6, 2)

        xb = [c.sb(f"xb{i}", [128, TPB, D], F32) for i in range(1)]
        hbuf = c.sb("hbuf", [128, TPB, D], BF16)
        hT = c.sb("hT", [128, 8, BLK], BF16)
        junk = c.sb("junk", [128, D], BF16)
        ss = c.sb("ss", [128, TPB], F32)
        rs = c.sb("rs", [128, TPB], F32)
        mhalf = c.sb("mhalf", [128, 512], F32)
        g1t = c.sb("g1t", [128, D], F32)
        rot = c.sb("rot_sb", [128, TPB, 2, 64], F32)
        ident = c.sb("ident", [128, 128], BF16)
        identf = c.sb("identf", [128, 128], F32)
        wbin = [c.sb(f"wbin{i}", [128, 8, 512], BF16) for i in range(2)]
        krot = c.sb("krot", [128, TPB, 512], BF16)
        vdec = c.sb("vdec", [128, TPB, 512], BF16)
        tmpA = [c.sb(f"tmpA{i}", [128, 512], F32) for i in range(1)]
        tmpB = [c.sb(f"tmpB{i}", [128, 512], F32) for i in range(1)]
        tb = [c.sb(f"tb{i}", [128, 512], F32) for i in range(2)]

        c.dma(g1t[:], g1_d.partition_broadcast(128), [], ["g1t"])
        s.add("pool", lambda e: e.memset(mhalf[:], -0.5), writes=["mhalf"])
        s.add("pool", lambda e: e.memset(identf[:], 0.0), writes=["identf"])
        s.add("pool", lambda e: e.affine_select(out=identf[:], in_=identf[:], pattern=[[-1, 128]],
                                               compare_op=ALU.not_equal, fill=1.0, base=0, channel_multiplier=1),
              reads=["identf"], writes=["identf"])
        s.add("dve", lambda e: e.tensor_copy(out=ident[:], in_=identf[:]), reads=["identf"], writes=["ident"])

        groups = (3, 4, 0, 1) if kind == "A" else (0, 1, 2, 3, 4, 5)
        for g in groups:
            c.dma(win_s[g], win_d[:, g * 512:(g + 1) * 512].rearrange("(ft p) c -> p ft c", p=128),
                  [], [f"win_s{g}"], eng="pool")

        if kind == "A":
            d1t = c.sb("d1t", [128, ntile, NH], F32)
            c.dma(d1t[:], d1_d, [], ["d1t"])
            sacc_t, sacc_k = c.banks.pop()
            sout = c.sb("sout", [128, 512], F32)
            utl = c.sb("utl", [128, 4, 128], F32)
        else:
            for kt2 in range(2):
                c.dma(wout_s[:, kt2 * 4:(kt2 + 1) * 4, :],
                      wout_d[kt2 * 512:(kt2 + 1) * 512, :].rearrange("(kt p) c -> p kt c", p=128),
                      [], [f"wout_s{kt2}"], eng="pool")
            for g in range(11):
                c.dma(wg_s[g], wg_d[:, g * 256:(g + 1) * 256].rearrange("(ft p) c -> p ft c", p=128),
                      [], [f"wg_s{g}"], eng="pool")
                c.dma(wu_s[g], wu_d[:, g * 256:(g + 1) * 256].rearrange("(ft p) c -> p ft c", p=128),
                      [], [f"wu_s{g}"], eng="pool")
            for f in range(NF):
                c.dma(wd_s[:, f, :], wd_d[f * 128:(f + 1) * 128, :], [], [f"wd_s{f}"], eng="pool")

            g2t = c.sb("g2t", [128, D], F32)
            gngt = c.sb("gngt", [128, RW], F32)
            cwT = c.sb("cwT", [128, 4, CK], F32)
            cbt = c.sb("cbt", [128, 4], F32)
            lngt = c.sb("lngt", [128, 4], F32)
            lnbt = c.sb("lnbt", [128, 4], F32)
            ctab = c.sb("ctab_sb", [128, 512], F32)
            maskt = c.sb("mask_sb", [128, 512], F32)
            dtab = c.sb("dtab_sb", [128, NH], F32)
            onesf = c.sb("onesf", [128, 128], F32)
            c.dma(g2t[:], g2_d.partition_broadcast(128), [], ["g2t"])
            c.dma(gngt[:], gng_d.partition_broadcast(128), [], ["gngt"])
            c.dma(cwT[:], cwT_d, [], ["cwT"])
            c.dma(cbt[:], cb_d, [], ["cbt"])
            c.dma(lngt[:], lng_d, [], ["lngt"])
            c.dma(lnbt[:], lnb_d, [], ["lnbt"])
            c.dma(ctab[:], ctab_d, [], ["ctab"])
            c.dma(maskt[:], mask_d, [], ["maskt"])
            c.dma(dtab[:], dtab_d, [], ["dtab"])
            s.add("pool", lambda e: e.memset(onesf[:], 1.0), writes=["onesf"])
            s.add("dve", lambda e: e.tensor_scalar(out=cwT[:], in0=cwT[:], scalar1=0.5, scalar2=None, op0=ALU.mult),
                  reads=["cwT"], writes=["cwT"])
            if final:
                gft = c.sb("gft", [128, D], F32)
                c.dma(gft[:], gf_d.partition_broadcast(128), [], ["gft"])
                pass

            wgu = [c.sb(f"wgu{i}", [128, 2, 8, 256], BF16) for i in range(2)]
            wdb = [c.sb(f"wdb{i}", [128, 2, 512], BF16) for i in range(3)]
            u2 = c.sb("u2", [128, 4, HALO + BLK], F32)
            acc = c.sb("acc", [128, 4, BLK], F32)
            ysq = c.sb("ysq", [128, 2, BLK], F32)
            mean = c.sb("mean", [128, BLK], F32)
            var = c.sb("var", [128, BLK], F32)
            qrot = c.sb("qrot", [128, TPB, 512], BF16)
            sg = acc
            qdT = [c.sb(f"qdT{i}", [128, 512], BF16) for i in range(2)]
            kT = [c.sb(f"kT{i}", [128, 512], BF16) for i in range(2)]
            sT = [c.sb(f"sT{i}", [128, 512], BF16) for i in range(2)]
            S = c.sb("S", [128, 512], F32)
            Sbf = c.sb("Sbf", [128, 512], BF16)
            osq = c.sb("osq", [128, 512], F32)
            rn = c.sb("rn", [128, 512], F32)
            gs = c.sb("gs", [128, 512], F32)
            rtok = c.sb("rtok", [128, 512], BF16)
            st1 = c.sb("st1", [128, NH], F32)
            st2 = c.sb("st2", [128, NH], F32)
            gmean = c.sb("gmean", [128, NH], F32)
            gvar = c.sb("gvar", [128, NH], F32)
            actT = c.sb("actT", [128, NF, BLK], BF16)
            mixT = actT
            sgt = tb
            c.dma(S[:], sin_d.rearrange("p h e -> p (h e)"), [], ["S"])
            c.act(Sbf[:], S[:], AF.Copy, ["S"], ["Sbf"])
            c.dma(u2[:, :, 0:HALO], uhalo_d, [], ["u2h"])

        def norm_to_hT(xt, xkeys, gt, gkey):
            for t in range(TPB):
                c.act(junk[:], xt[:, t, :], AF.Square, [xkeys[t]], ["junk", "ss"], accum_out=ss[:, t:t + 1])
            c.ts(rs[:], ss[:], 1.0 / D, EPS, ALU.mult, ALU.add, ["ss"], ["rs"])
            c.tt(rs[:], rs[:], mhalf[:, 0:TPB], ALU.pow, ["rs", "mhalf"], ["rs"], eng="pool")
            for t in range(TPB):
                c.stt(hbuf[:, t, :], xt[:, t, :], rs[:, t:t + 1], gt[:], ALU.mult, ALU.mult,
                      [xkeys[t], "rs", gkey], [f"h{t}"])
            for ft in range(8):
                pt, pk = c.tbank()
                for t in range(TPB):
                    c.tr(pt[:, t * 128:(t + 1) * 128], hbuf[:, t, ft * 128:(ft + 1) * 128], ident[:],
                         [f"h{t}", "ident"], [pk])
                c.act(hT[:, ft, :], pt[:, 0:BLK], AF.Copy, [pk], [f"hT{ft}"])

        hTkeys = [f"hT{ft}" for ft in range(8)]
        wslot = [0]

        def load_win(g):
            i = wslot[0] % 2
            wslot[0] += 1
            c.dma(wbin[i][:], win_s[g], [f"win_s{g}"], [f"wbin{i}"])
            return wbin[i], f"wbin{i}"

        def proj_tok(w, wk, t, pb, pk):
            for ft in range(8):
                c.mm(pb[:], hT[:, ft, t * 128:(t + 1) * 128], w[:, ft, :], ft == 0, ft == 7,
                     [f"hT{ft}", wk], [pk])

        def rotary(pb, pk, tile, out, outkey, i):
            p4 = pb[:].rearrange("p (h two d) -> p h two d", h=NH, two=2)
            tl = tile % TPB
            i = 0
            cosb = rot[:, tl, 0, :].unsqueeze(1).unsqueeze(1).to_broadcast([128, NH, 2, 64])
            sinb = rot[:, tl, 1, :].unsqueeze(1).to_broadcast([128, NH, 64])
            a4 = tmpA[i][:].rearrange("p (h two d) -> p h two d", h=NH, two=2)
            b4 = tmpB[i][:].rearrange("p (h two d) -> p h two d", h=NH, two=2)
            o4 = out.rearrange("p (h two d) -> p h two d", h=NH, two=2)
            c.tt(a4, p4, cosb, ALU.mult, [pk, "rot"], [f"tmpA{i}"])
            c.tt(b4[:, :, 0, :], p4[:, :, 1, :], sinb, ALU.mult, [pk, "rot"], [f"tmpB{i}"])
            c.tt(b4[:, :, 1, :], p4[:, :, 0, :], sinb, ALU.mult, [pk, "rot"], [f"tmpB{i}"])
            c.tt(o4[:, :, 0, :], a4[:, :, 0, :], b4[:, :, 0, :], ALU.subtract, [f"tmpA{i}", f"tmpB{i}"], [outkey])
            c.tt(o4[:, :, 1, :], a4[:, :, 1, :], b4[:, :, 1, :], ALU.add, [f"tmpA{i}", f"tmpB{i}"], [outkey])

        def glu_cols(wa, wak, wb_, wbk, ct, tok0, ntk, outap, outkey, i):
            pa, pak = c.bank()
            for ft in range(8):
                c.mm(pa[:, 0:ntk], wa[:, ft, ct * 128:(ct + 1) * 128], hT[:, ft, tok0:tok0 + ntk], ft == 0, ft == 7,
                     [f"hT{ft}", wak], [pak])
            pb, pbk = c.bank()
            for ft in range(8):
                c.mm(pb[:, 0:ntk], wb_[:, ft, ct * 128:(ct + 1) * 128], hT[:, ft, tok0:tok0 + ntk], ft == 0, ft == 7,
                     [f"hT{ft}", wbk], [pbk])
            c.act(tb[i][:, 0:ntk], pb[:, 0:ntk], AF.Tanh, [pbk], [f"tb{i}"], scale=0.5)
            c.stt(outap, tb[i][:, 0:ntk], 1.0, pa[:, 0:ntk], ALU.add, ALU.mult, [f"tb{i}", pak], [outkey])

        for b in range(nblk):
            xt = xb[0]
            xk = [f"xb0_{t}" for t in range(TPB)]
            c.dma(xt[:], x_d[:, b * TPB:(b + 1) * TPB, :], [], xk)
            c.dma(rot[:], rot_d[:, b * TPB:(b + 1) * TPB], [], ["rot"])

            if kind == "A":
                norm_to_hT(xt, xk, g1t, "g1t")
                w, wk = load_win(3)
                for t in range(TPB):
                    pb, pk = c.bank()
                    proj_tok(w, wk, t, pb, pk)
                    rotary(pb, pk, b * TPB + t, krot[:, t, :], f"krot{t}", t % 2)
                w, wk = load_win(4)
                for t in range(TPB):
                    pb, pk = c.bank()
                    proj_tok(w, wk, t, pb, pk)
                    tile = b * TPB + t
                    c.tt(vdec[:, t, :].rearrange("p (h e) -> p h e", h=NH),
                         pb[:].rearrange("p (h e) -> p h e", h=NH),
                         d1t[:, tile, :].unsqueeze(2).to_broadcast([128, NH, DH]), ALU.mult,
                         [pk, "d1t"], [f"vdec{t}"])
                for t in range(TPB):
                    tile = b * TPB + t
                    for h in range(NH):
                        hs = slice(h * DH, (h + 1) * DH)
                        c.mm(sacc_t[:, hs], krot[:, t, hs], vdec[:, t, hs], tile == 0, tile == ntile - 1,
                             [f"krot{t}", f"vdec{t}"], [sacc_k])
                if b == nblk - 1:
                    wa, wak = load_win(0)
                    wb_, wbk = load_win(1)
                    for ct in range(4):
                        glu_cols(wa, wak, wb_, wbk, ct, BLK - 128, 128, utl[:, ct, :], "utl", ct % 2)
                    c.dma(utail_d, utl[:, :, 128 - HALO:128], ["utl"], ["utail_d"])
                    c.act(sout[:], sacc_t[:], AF.Copy, [sacc_k], ["sout"])
                    c.dma(send_d.rearrange("p h e -> p (h e)"), sout[:], ["sout"], ["send_d"])
                continue

            norm_to_hT(xt, xk, g1t, "g1t")
            wa, wak = load_win(0)
            wb_, wbk = load_win(1)
            for ct in range(4):
                glu_cols(wa, wak, wb_, wbk, ct, 0, BLK, u2[:, ct, HALO:HALO + BLK], f"u2b{ct}", ct % 2)
            for k in range(CK):
                for ct in range(4):
                    rd = [f"u2b{ct}", "u2h", "cwT"]
                    if k == 0:
                        s.add("dve", lambda e, k=k, ct=ct: e.tensor_scalar(
                            out=acc[:, ct, :], in0=u2[:, ct, k:k + BLK], scalar1=cwT[:, ct, k:k + 1],
                            scalar2=cbt[:, ct:ct + 1], op0=ALU.mult, op1=ALU.add),
                            reads=rd + ["cbt"], writes=[f"acc{ct}"])
                    else:
                        c.stt(acc[:, ct, :], u2[:, ct, k:k + BLK], cwT[:, ct, k:k + 1], acc[:, ct, :],
                              ALU.mult, ALU.add, rd + [f"acc{ct}"], [f"acc{ct}"])
            for ct in range(4):
                c.act(u2[:, ct, 0:HALO], u2[:, ct, BLK:BLK + HALO], AF.Copy, [f"u2b{ct}"], ["u2h"])
            pm1, pm1k = c.bank()
            for ct in range(4):
                c.mm(pm1[:], onesf[:], acc[:, ct, :], ct == 0, ct == 3, ["onesf", f"acc{ct}"], [pm1k])
            pm2, pm2k = c.bank()
            for ct in range(4):
                c.act(ysq[:, ct % 2, :], acc[:, ct, :], AF.Square, [f"acc{ct}"], [f"ysq{ct % 2}"])
                c.mm(pm2[:], onesf[:], ysq[:, ct % 2, :], ct == 0, ct == 3, ["onesf", f"ysq{ct % 2}"], [pm2k])
            c.ts(mean[:], pm1[:], 1.0 / CW, None, ALU.mult, None, [pm1k], ["mean"])
            c.tt(var[:], mean[:], mean[:], ALU.mult, ["mean"], ["var"])
            c.stt(var[:], pm2[:], 1.0 / CW, var[:], ALU.mult, ALU.subtract, [pm2k, "var"], ["var"])
            c.ts(var[:], var[:], EPS, None, ALU.add, None, ["var"], ["var"])
            c.tt(var[:], var[:], mhalf[:], ALU.pow, ["var", "mhalf"], ["var"], eng="pool")
            for ct in range(4):
                c.tt(acc[:, ct, :], acc[:, ct, :], mean[:], ALU.subtract, [f"acc{ct}", "mean"], [f"acc{ct}"])
                c.tt(acc[:, ct, :], acc[:, ct, :], var[:], ALU.mult, [f"acc{ct}", "var"], [f"acc{ct}"])
                c.act(mixT[:, ct, :], acc[:, ct, :], AF.Silu, [f"acc{ct}", "lngt", "lnbt"], [f"actT{ct}"],
                      scale=lngt[:, ct:ct + 1], bias=lnbt[:, ct:ct + 1])

            w, wk = load_win(2)
            for t in range(TPB):
                pb, pk = c.bank()
                proj_tok(w, wk, t, pb, pk)
                rotary(pb, pk, b * TPB + t, qrot[:, t, :], f"qrot{t}", t % 2)
            w, wk = load_win(3)
            for t in range(TPB):
                pb, pk = c.bank()
                proj_tok(w, wk, t, pb, pk)
                rotary(pb, pk, b * TPB + t, krot[:, t, :], f"krot{t}", t % 2)
            w, wk = load_win(4)
            for t in range(TPB):
                pb, pk = c.bank()
                proj_tok(w, wk, t, pb, pk)
                c.tt(vdec[:, t, :].rearrange("p (h e) -> p h e", h=NH),
                     pb[:].rearrange("p (h e) -> p h e", h=NH),
                     dtab[:].unsqueeze(2).to_broadcast([128, NH, DH]), ALU.mult,
                     [pk, "dtab"], [f"vdec{t}"])
            w, wk = load_win(5)
            for t in range(TPB):
                pb, pk = c.bank()
                proj_tok(w, wk, t, pb, pk)
                c.act(sg[:, t, :], pb[:], AF.Silu, [pk], [f"acc{t}"])

            for t in range(TPB):
                i = t % 2
                pt, ptk = c.tbank()
                for h in range(NH):
                    hs = slice(h * DH, (h + 1) * DH)
                    c.tr(pt[:, hs], qrot[:, t, hs], ident[:], [f"qrot{t}", "ident"], [ptk])
                c.tt(qdT[i][:], pt[:, 0:512], ctab[:], ALU.mult, [ptk, "ctab"], [f"qdT{i}"])
                pt, ptk = c.tbank()
                for h in range(NH):
                    hs = slice(h * DH, (h + 1) * DH)
                    c.tr(pt[:, hs], krot[:, t, hs], ident[:], [f"krot{t}", "ident"], [ptk])
                c.act(kT[i][:], pt[:, 0:512], AF.Copy, [ptk], [f"kT{i}"])
                psc, psck = c.bank()
                for h in range(NH):
                    hs = slice(h * DH, (h + 1) * DH)
                    c.mm(psc[:, hs], kT[i][:, hs], qdT[i][:, hs], True, True, [f"kT{i}", f"qdT{i}"], [psck])
                c.tt(sT[i][:], psc[:], maskt[:], ALU.mult, [psck, "maskt"], [f"sT{i}"])
                po, pok = c.bank()
                for h in range(NH):
                    hs = slice(h * DH, (h + 1) * DH)
                    c.mm(po[:, hs], sT[i][:, hs], vdec[:, t, hs], True, False, [f"sT{i}", f"vdec{t}"], [pok])
                    c.mm(po[:, hs], qdT[i][:, hs], Sbf[:, hs], False, True, [f"qdT{i}", "Sbf"], [pok])
                pst, pstk = c.bank()
                for h in range(NH):
                    hs = slice(h * DH, (h + 1) * DH)
                    c.mm(pst[:, hs], krot[:, t, hs], vdec[:, t, hs], True, True, [f"krot{t}", f"vdec{t}"], [pstk])
                for h in range(NH):
                    hs = slice(h * DH, (h + 1) * DH)
                    c.stt(S[:, hs], S[:, hs], float(GAM[h] ** 128), pst[:, hs], ALU.mult, ALU.add,
                          ["S", pstk], ["S"])
                c.act(Sbf[:], S[:], AF.Copy, ["S"], ["Sbf"])
                po3 = po[:].rearrange("p (h e) -> p h e", h=NH)
                s.add("dve", lambda e, po3=po3: e.tensor_reduce(out=st1[:], in_=po3, axis=AX.X, op=ALU.add),
                      reads=[pok], writes=["st1"])
                c.act(osq[:], po[:], AF.Square, [pok], ["osq"])
                s.add("dve", lambda e: e.tensor_reduce(out=st2[:], in_=osq[:].rearrange("p (h e) -> p h e", h=NH),
                                                       axis=AX.X, op=ALU.add),
                      reads=["osq"], writes=["st2"])
                c.ts(gmean[:], st1[:], 1.0 / DH, None, ALU.mult, None, ["st1"], ["gmean"])
                c.tt(gvar[:], gmean[:], gmean[:], ALU.mult, ["gmean"], ["gvar"])
                c.stt(gvar[:], st2[:], 1.0 / DH, gvar[:], ALU.mult, ALU.subtract, ["st2", "gvar"], ["gvar"])
                c.ts(gvar[:], gvar[:], EPS, None, ALU.add, None, ["gvar"], ["gvar"])
                c.tt(gvar[:], gvar[:], mhalf[:, 0:NH], ALU.pow, ["gvar", "mhalf"], ["gvar"], eng="pool")
                for h in range(NH):
                    hs = slice(h * DH, (h + 1) * DH)
                    c.ts(rn[:, hs], po[:, hs], gmean[:, h:h + 1], gvar[:, h:h + 1], ALU.subtract, ALU.mult,
                         [pok, "gmean", "gvar"], ["rn"])
                c.tt(gs[:], sg[:, t, :], gngt[:], ALU.mult, [f"acc{t}", "gngt"], ["gs"], eng="pool")
                c.tt(rtok[:], rn[:], gs[:], ALU.mult, ["rn", "gs"], ["rtok"])
                pt, ptk = c.tbank()
                for h in range(NH):
                    hs = slice(h * DH, (h + 1) * DH)
                    c.tr(pt[:, hs], rtok[:, hs], ident[:], ["rtok", "ident"], [ptk])
                for h in range(NH):
                    hs = slice(h * DH, (h + 1) * DH)
                    c.act(mixT[:, 4 + h, t * 128:(t + 1) * 128], pt[:, hs], AF.Copy, [ptk], [f"actT{4 + h}"])

            for n in range(2):
                wi = wslot[0] % 2
                wslot[0] += 1
                c.dma(wbin[wi][:], wout_s[:, :, n * 512:(n + 1) * 512], ["wout_s0", "wout_s1"], [f"wbin{wi}"])
                for t in range(TPB):
                    pb, pk = c.bank()
                    for kt in range(8):
                        c.mm(pb[:], mixT[:, kt, t * 128:(t + 1) * 128], wbin[wi][:, kt, :],
                             kt == 0, kt == 7, [f"actT{kt}", f"wbin{wi}"], [pk])
                    c.tt(xt[:, t, n * 512:(n + 1) * 512], xt[:, t, n * 512:(n + 1) * 512], pb[:], ALU.add,
                         [xk[t], pk], [xk[t]])

            norm_to_hT(xt, xk, g2t, "g2t")
            for g in range(11):
                i = g % 2
                c.dma(wgu[i][:, 0], wg_s[g], [f"wg_s{g}"], [f"wgu{i}g"])
                c.dma(wgu[i][:, 1], wu_s[g], [f"wu_s{g}"], [f"wgu{i}u"])
                for fi in range(2):
                    f = 2 * g + fi
                    pg, pgk = c.bank()
                    for ft in range(8):
                        c.mm(pg[:], wgu[i][:, 0, ft, fi * 128:(fi + 1) * 128], hT[:, ft, :], ft == 0, ft == 7,
                             [f"hT{ft}", f"wgu{i}g"], [pgk])
                    pu, puk = c.bank()
                    for ft in range(8):
                        c.mm(pu[:], wgu[i][:, 1, ft, fi * 128:(fi + 1) * 128], hT[:, ft, :], ft == 0, ft == 7,
                             [f"hT{ft}", f"wgu{i}u"], [puk])
                    c.act(sgt[fi][:], pg[:], AF.Silu, [pgk], [f"tb{fi}"])
                    c.tt(actT[:, f, :], sgt[fi][:], pu[:], ALU.mult, [f"tb{fi}", puk], [f"actT{f}"])
            dslot = 0
            for n in range(2):
                pds = [c.bank() for _ in range(TPB)]
                for f2 in range(NF // 2):
                    i = dslot % 3
                    dslot += 1
                    c.dma(wdb[i][:], wd_s[:, 2 * f2:2 * f2 + 2, n * 512:(n + 1) * 512],
                          [f"wd_s{2 * f2}", f"wd_s{2 * f2 + 1}"], [f"wdb{i}"])
                    for fi in range(2):
                        f = 2 * f2 + fi
                        for t in range(TPB):
                            c.mm(pds[t][0][:], actT[:, f, t * 128:(t + 1) * 128], wdb[i][:, fi, :],
                                 f == 0, f == NF - 1, [f"actT{f}", f"wdb{i}"], [pds[t][1]])
                for t in range(TPB):
                    c.tt(xt[:, t, n * 512:(n + 1) * 512], xt[:, t, n * 512:(n + 1) * 512], pds[t][0][:], ALU.add,
                         [xk[t], pds[t][1]], [xk[t]])

            if not final:
                c.dma(xo_d[:, b * TPB:(b + 1) * TPB, :], xt[:], xk, [f"xo{b}"])
            else:
                for t in range(TPB):
                    c.act(junk[:], xt[:, t, :], AF.Square, [xk[t]], ["junk", "ss"], accum_out=ss[:, t:t + 1])
                c.ts(rs[:], ss[:], 1.0 / D, EPS, ALU.mult, ALU.add, ["ss"], ["rs"])
                c.tt(rs[:], rs[:], mhalf[:, 0:TPB], ALU.pow, ["rs", "mhalf"], ["rs"], eng="pool")
                for t in range(TPB):
                    ob = acc[:, 2 * (t % 2):2 * (t % 2) + 2, :].rearrange("p a f -> p (a f)")
                    obk = [f"acc{2 * (t % 2)}", f"acc{2 * (t % 2) + 1}"]
                    c.stt(ob, xt[:, t, :], rs[:, t:t + 1], gft[:], ALU.mult, ALU.mult,
                          [xk[t], "rs", "gft"], obk)
                    c.dma(xo_d[:, b * TPB + t, :], ob, obk, [f"xo{b}_{t}"])

        outkeys = [k for k in s.lastw if k.startswith("xo") or k in ("utail_d", "send_d")]
        s.add("sp", lambda e: e.nop(), reads=outkeys)
        s.emit(st)
    return nc


def _tables(ntok, tok_off):
    ntile = ntok // 128
    half = DH // 2
    freqs = (np.float32(10000.0) ** (-np.arange(half, dtype=np.float32) / np.float32(half))).astype(np.float32)
    pos = (np.arange(ntok, dtype=np.float32) + np.float32(tok_off)).astype(np.float32)
    ang = (pos[:, None] * freqs[None, :]).astype(np.float32)
    cos = np.cos(ang).astype(np.float32)
    sin = np.sin(ang).astype(np.float32)
    rot = np.stack([cos, sin], axis=1)
    rot = rot.reshape(ntile, 128, 2, 64).transpose(1, 0, 2, 3).copy()
    gam = np.array(GAM, dtype=np.float64)
    scale = DH ** -0.5
    i = np.arange(128)
    ctab = np.stack([gam[h] ** (i + 1.0) for h in range(NH)], 0)
    ctab = np.broadcast_to(ctab.reshape(1, NH * 128), (128, NH * 128)).astype(np.float32).copy()
    jj = i[:, None]
    ii = i[None, :]
    ind = ((jj // 64) <= (ii // 64)).astype(np.float64)
    mask = np.stack([gam[h] ** (np.abs(ii - jj) - ii + jj - 128.0) * ind for h in range(NH)], 1)
    mask = mask.reshape(128, NH * 128).astype(np.float32)
    dtab = np.stack([gam[h] ** (127.0 - i) * scale for h in range(NH)], 1).astype(np.float32)
    j = np.arange(ntok, dtype=np.float64)
    d1 = np.stack([gam[h] ** (ntok - 1.0 - j) * scale for h in range(NH)], 1)
    d1 = d1.reshape(ntile, 128, NH).transpose(1, 0, 2).astype(np.float32).copy()
    return rot, ctab, mask, dtab, d1


_PROGS = {}


def _prog(kind, ntok, final=False):
    key = (kind, ntok, final)
    if key not in _PROGS:
        _PROGS[key] = build(kind, ntok, final)
    return _PROGS[key]


def _chanvec(v):
    return np.ascontiguousarray(v.reshape(4, 128).T)


def run_model(x, norm1_g, w_in, conv_w, conv_b, conv_ln_g, conv_ln_b, ret_gn_g, w_out,
              norm2_g, w_gate, w_up, w_down, final_g, depth=DEPTH):
    Bsz, S, _ = x.shape
    ncore = 2 * Bsz
    ntok = S // 2
    f32 = lambda a: np.ascontiguousarray(np.asarray(a, dtype=np.float32))
    tabs = [_tables(ntok, (cix % 2) * ntok) for cix in range(ncore)]
    xs = [f32(x[cix // 2, (cix % 2) * ntok:(cix % 2 + 1) * ntok, :]) for cix in range(ncore)]
    cores = list(range(ncore))
    for l in range(depth):
        pa = _prog("A", ntok)
        maps = [{"x": xs[cix], "g1": f32(norm1_g[l]).reshape(1, D), "w_in": f32(w_in[l]),
                 "rot": tabs[cix][0], "d1tab": tabs[cix][4]} for cix in cores]
        ra = run_bass_kernel_spmd(pa, maps, core_ids=cores).results
        last = l == depth - 1
        pb = _prog("B", ntok, last)
        maps = []
        for cix in cores:
            if cix % 2 == 1:
                s_in = np.ascontiguousarray(ra[cix - 1]["s_end"])
                u_h = np.ascontiguousarray(ra[cix - 1]["u_tail"])
            else:
                s_in = np.zeros((128, NH, DH), np.float32)
                u_h = np.zeros((128, 4, HALO), np.float32)
            m = {"x": xs[cix], "s_in": s_in, "u_halo": u_h,
                 "g1": f32(norm1_g[l]).reshape(1, D), "g2": f32(norm2_g[l]).reshape(1, D),
                 "w_in": f32(w_in[l]),
                 "conv_wT": np.ascontiguousarray(f32(conv_w[l]).reshape(CK, 4, 128).transpose(2, 1, 0)),
                 "conv_b": _chanvec(f32(conv_b[l])), "ln_g": _chanvec(f32(conv_ln_g[l])),
                 "ln_b": _chanvec(f32(conv_ln_b[l])), "gn_g": f32(ret_gn_g[l]).reshape(1, RW),
                 "w_out": f32(w_out[l]), "w_gate": f32(w_gate[l]), "w_up": f32(w_up[l]),
                 "w_down": f32(w_down[l]),
                 "rot": tabs[cix][0], "ctab": tabs[cix][1], "mask": tabs[cix][2], "dtab": tabs[cix][3]}
            if last:
                m["gf"] = f32(final_g).reshape(1, D)
            maps.append(m)
        rb = run_bass_kernel_spmd(pb, maps, core_ids=cores).results
        xs = [np.ascontiguousarray(rb[cix]["x_out"]) for cix in cores]
    out = np.empty((Bsz, S, D), np.float32)
    for cix in cores:
        out[cix // 2, (cix % 2) * ntok:(cix % 2 + 1) * ntok, :] = xs[cix]
    return out


def kernel(**inputs):
    return run_model(**inputs)
```
